# Optimizing a Trainium2 kernel written in Bass

```python
import jax, jax.numpy as jnp
from jax import lax
import numpy as np

D_MODEL = 1024
BATCH = 4
SEQ = 4096
DEPTH = 2

BLOCK_Q = 128
EPS = 1e-6
NEG_INF = -1e30
ROPE_THETA = 10000.0

MLA_HEADS = 8
MLA_Q_RANK = 256
MLA_KV_RANK = 128
MLA_NOPE = 64
MLA_ROPE = 32
MLA_QK = MLA_NOPE + MLA_ROPE
MLA_V = 64
SB_HEADS = 8
SB_DIM = 64
SB_WIDTH = SB_HEADS * SB_DIM
FOX_HEADS = 16
FOX_DIM = 64
FOX_WIDTH = FOX_HEADS * FOX_DIM
D_FF = ((8 * D_MODEL // 3 + 255) // 256) * 256

EVEN_IN = MLA_Q_RANK + MLA_KV_RANK + MLA_ROPE + 3 * SB_WIDTH
EVEN_OUT = MLA_HEADS * MLA_V + SB_WIDTH
ODD_IN = 3 * FOX_WIDTH + FOX_HEADS
ODD_OUT = FOX_WIDTH

kernel_name = 'hybrid_mla_stickbreaking_fox_block'


def rms_norm(x, g):
    xf = x.astype(jnp.float32)
    y = xf * lax.rsqrt(jnp.mean(xf * xf, axis=-1, keepdims=True) + EPS)
    return (y * g.astype(jnp.float32)).astype(x.dtype)


def split_last(x, sizes):
    return jnp.split(x, [int(v) for v in np.cumsum(sizes)[:-1]], axis=-1)


def heads_first(x):
    return x.transpose(0, 2, 1, 3)


def merge_heads(x):
    b, h, s, d = x.shape
    return x.transpose(0, 2, 1, 3).reshape(b, s, h * d)


def rope_tail(x, positions):
    nope, r = x[..., :-MLA_ROPE], x[..., -MLA_ROPE:]
    half = MLA_ROPE // 2
    inv_freq = ROPE_THETA ** (-jnp.arange(half, dtype=jnp.float32) / half)
    ang = positions.astype(jnp.float32)[..., None] * inv_freq
    cos, sin = jnp.cos(ang)[:, :, None, :], jnp.sin(ang)[:, :, None, :]
    r1, r2 = r[..., :half].astype(jnp.float32), r[..., half:].astype(jnp.float32)
    rot = jnp.concatenate([r1 * cos - r2 * sin, r2 * cos + r1 * sin], axis=-1)
    return jnp.concatenate([nope, rot.astype(x.dtype)], axis=-1)


def sweep_query_blocks(block_fn, q):
    b, h, s, _ = q.shape
    out = lax.map(block_fn, jnp.arange(s // BLOCK_Q))
    nb, _, _, bq, dv = out.shape
    return out.transpose(1, 2, 0, 3, 4).reshape(b, h, nb * bq, dv)


def causal_softmax_attention(q, k, v, scale, decay=None):
    s = k.shape[2]
    k_idx = jnp.arange(s)

    def block(i):
        start = i * BLOCK_Q
        qb = lax.dynamic_slice_in_dim(q, start, BLOCK_Q, axis=2)
        logits = jnp.einsum('bhqd,bhkd->bhqk', qb, k).astype(jnp.float32) * scale
        if decay is not None:
            cq = lax.dynamic_slice_in_dim(decay, start, BLOCK_Q, axis=2)
            logits = logits + cq[..., :, None] - decay[..., None, :]
        q_idx = start + jnp.arange(BLOCK_Q)
        mask = k_idx[None, :] <= q_idx[:, None]
        p = jax.nn.softmax(jnp.where(mask, logits, NEG_INF), axis=-1)
        return jnp.einsum('bhqk,bhkd->bhqd', p.astype(v.dtype), v)

    return sweep_query_blocks(block, q)


def stick_breaking_attention(q, k, v):
    s = k.shape[2]
    scale = q.shape[-1] ** -0.5
    k_idx = jnp.arange(s)

    def block(i):
        start = i * BLOCK_Q
        qb = lax.dynamic_slice_in_dim(q, start, BLOCK_Q, axis=2)
        z = jnp.einsum('bhqd,bhkd->bhqk', qb, k).astype(jnp.float32) * scale
        q_idx = start + jnp.arange(BLOCK_Q)
        mask = k_idx[None, :] < q_idx[:, None]
        log_beta = jax.nn.log_sigmoid(z)
        log_1m_beta = jnp.where(mask, jax.nn.log_sigmoid(-z), 0.0)
        suffix = lax.cumsum(log_1m_beta, axis=3, reverse=True) - log_1m_beta
        a = jnp.where(mask, jnp.exp(log_beta + suffix), 0.0)
        return jnp.einsum('bhqk,bhkd->bhqd', a.astype(v.dtype), v)

    return sweep_query_blocks(block, q)


def even_mixer(h, positions, w_in, q_a_norm, w_q_up, kv_a_norm, w_kv_up, q_norm, k_norm, w_o):
    b, s, _ = h.shape
    proj = h @ w_in
    cq, ckv, k_rope, sb_qkv = split_last(proj, [MLA_Q_RANK, MLA_KV_RANK, MLA_ROPE, 3 * SB_WIDTH])
    q = (rms_norm(cq, q_a_norm) @ w_q_up).reshape(b, s, MLA_HEADS, MLA_QK)
    kv = (rms_norm(ckv, kv_a_norm) @ w_kv_up).reshape(b, s, MLA_HEADS, MLA_NOPE + MLA_V)
    k_nope, v = kv[..., :MLA_NOPE], kv[..., MLA_NOPE:]
    k_rope = jnp.broadcast_to(k_rope[:, :, None, :], (b, s, MLA_HEADS, MLA_ROPE))
    k = jnp.concatenate([k_nope, k_rope], axis=-1)
    q = rope_tail(rms_norm(q, q_norm), positions)
    k = rope_tail(rms_norm(k, k_norm), positions)
    mla = causal_softmax_attention(heads_first(q), heads_first(k), heads_first(v), MLA_QK ** -0.5)
    sq, sk, sv = [heads_first(t.reshape(b, s, SB_HEADS, SB_DIM)) for t in jnp.split(sb_qkv, 3, axis=-1)]
    sb = stick_breaking_attention(sq, sk, sv)
    o = jnp.concatenate([merge_heads(mla), merge_heads(sb)], axis=-1)
    return o @ w_o


def odd_mixer(h, w_in, f_bias, q_norm, k_norm, w_o):
    b, s, _ = h.shape
    proj = h @ w_in
    q, k, v, f_logit = split_last(proj, [FOX_WIDTH, FOX_WIDTH, FOX_WIDTH, FOX_HEADS])
    q = rms_norm(q.reshape(b, s, FOX_HEADS, FOX_DIM), q_norm)
    k = rms_norm(k.reshape(b, s, FOX_HEADS, FOX_DIM), k_norm)
    v = v.reshape(b, s, FOX_HEADS, FOX_DIM)
    log_f = jax.nn.log_sigmoid(f_logit.astype(jnp.float32) + f_bias.astype(jnp.float32))
    decay = lax.cumsum(log_f, axis=1).transpose(0, 2, 1)
    o = causal_softmax_attention(heads_first(q), heads_first(k), heads_first(v), FOX_DIM ** -0.5, decay)
    return merge_heads(o) @ w_o


def swiglu(h, w_gate, w_up, w_down):
    return (jax.nn.silu(h @ w_gate) * (h @ w_up)) @ w_down


def setup_inputs(seed: int = 0) -> dict:
    key = jax.random.key(seed)
    ks = iter(jax.random.split(key, 32))

    def dense(shape):
        return jax.random.normal(next(ks), shape, jnp.float32) * shape[0] ** -0.5

    def gain(n):
        return 1.0 + 0.1 * jax.random.normal(next(ks), (n,), jnp.float32)

    x = jax.random.normal(next(ks), (BATCH, SEQ, D_MODEL), jnp.float32)
    offsets = jax.random.randint(next(ks), (BATCH, 1), 0, SEQ, dtype=jnp.int32)
    positions = (offsets + jnp.arange(SEQ, dtype=jnp.int32)[None, :]).astype(jnp.int32)
    return {
        'x': x,
        'positions': positions,
        'l0_attn_norm': gain(D_MODEL),
        'l0_w_in': dense((D_MODEL, EVEN_IN)),
        'l0_mla_q_a_norm': gain(MLA_Q_RANK),
        'l0_mla_w_q_up': dense((MLA_Q_RANK, MLA_HEADS * MLA_QK)),
        'l0_mla_kv_a_norm': gain(MLA_KV_RANK),
        'l0_mla_w_kv_up': dense((MLA_KV_RANK, MLA_HEADS * (MLA_NOPE + MLA_V))),
        'l0_mla_q_norm': gain(MLA_QK),
        'l0_mla_k_norm': gain(MLA_QK),
        'l0_w_o': dense((EVEN_OUT, D_MODEL)),
        'l0_ffn_norm': gain(D_MODEL),
        'l0_w_gate': dense((D_MODEL, D_FF)),
        'l0_w_up': dense((D_MODEL, D_FF)),
        'l0_w_down': dense((D_FF, D_MODEL)),
        'l1_attn_norm': gain(D_MODEL),
        'l1_w_in': dense((D_MODEL, ODD_IN)),
        'l1_fox_f_bias': 3.0 + 0.5 * jax.random.normal(next(ks), (FOX_HEADS,), jnp.float32),
        'l1_fox_q_norm': gain(FOX_DIM),
        'l1_fox_k_norm': gain(FOX_DIM),
        'l1_w_o': dense((ODD_OUT, D_MODEL)),
        'l1_ffn_norm': gain(D_MODEL),
        'l1_w_gate': dense((D_MODEL, D_FF)),
        'l1_w_up': dense((D_MODEL, D_FF)),
        'l1_w_down': dense((D_FF, D_MODEL)),
    }


def reference(x, positions,
              l0_attn_norm, l0_w_in, l0_mla_q_a_norm, l0_mla_w_q_up, l0_mla_kv_a_norm,
              l0_mla_w_kv_up, l0_mla_q_norm, l0_mla_k_norm, l0_w_o,
              l0_ffn_norm, l0_w_gate, l0_w_up, l0_w_down,
              l1_attn_norm, l1_w_in, l1_fox_f_bias, l1_fox_q_norm, l1_fox_k_norm, l1_w_o,
              l1_ffn_norm, l1_w_gate, l1_w_up, l1_w_down):
    mixer_params = (
        (l0_attn_norm, l0_w_in, l0_mla_q_a_norm, l0_mla_w_q_up, l0_mla_kv_a_norm,
         l0_mla_w_kv_up, l0_mla_q_norm, l0_mla_k_norm, l0_w_o),
        (l1_attn_norm, l1_w_in, l1_fox_f_bias, l1_fox_q_norm, l1_fox_k_norm, l1_w_o),
    )
    ffn_params = (
        (l0_ffn_norm, l0_w_gate, l0_w_up, l0_w_down),
        (l1_ffn_norm, l1_w_gate, l1_w_up, l1_w_down),
    )
    for layer in range(DEPTH):
        norm_g, *mp = mixer_params[layer]
        h = rms_norm(x, norm_g)
        if layer % 2 == 0:
            x = x + even_mixer(h, positions, *mp)
        else:
            x = x + odd_mixer(h, *mp)
        f_norm, w_gate, w_up, w_down = ffn_params[layer]
        x = x + swiglu(rms_norm(x, f_norm), w_gate, w_up, w_down)
    return x
```

```python
import contextlib
import numpy as np
import ml_dtypes

import concourse.bass as bass
import concourse.mybir as mybir
from concourse.bass_utils import run_bass_kernel_spmd

F32 = mybir.dt.float32
BF16 = mybir.dt.bfloat16
I32 = mybir.dt.int32
AF = mybir.ActivationFunctionType
ALU = mybir.AluOpType
AX = mybir.AxisListType

D = 1024
KC = 8
DFF = 2816
NFF = 22
EPS = 1e-6
NCORES = 8

C_ID, C_TRI, C_TRS, C_NTI, C_NON = 0, 128, 256, 384, 512
F_ONE, F_SEL, F_TRC, F_ALL, F_INV = 0, 64, 192, 320, 448
F_W = 464


class Tk:
    __slots__ = ("w", "r")

    def __init__(self):
        self.w = None
        self.r = {}


class Eng:
    def __init__(self, name, h, sem, is_pe=False):
        self.name, self.h, self.sem, self.n = name, h, sem, 0
        self.waited = {}
        self.is_pe = is_pe


class Sched:
    def __init__(self, nc, stack):
        self.nc = nc
        mk = lambda n: stack.enter_context(nc.semaphore(n))
        self.E = {
            "pe": Eng("pe", nc.tensor, mk("s_pe"), True),
            "act": Eng("act", nc.scalar, mk("s_act")),
            "dve": Eng("dve", nc.vector, mk("s_dve")),
            "pool": Eng("pool", nc.gpsimd, mk("s_pool")),
            "sp": Eng("sp", nc.sync, mk("s_sp")),
        }
        self.Q = {}
        for q, eng, n in (("sp", "sp", 24), ("pool", "pool", 12)):
            self.Q[q] = dict(eng=self.E[eng], sems=[mk(f"d_{q}{i}") for i in range(n)],
                             vals=[0] * n, i=0)
        self.sems = {}
        for e in self.E.values():
            self.sems[id(e.sem)] = e.sem
        for q in self.Q.values():
            for s in q["sems"]:
                self.sems[id(s)] = s

    def _deps(self, reads, writes):
        deps = []
        for t in reads:
            if t.w is not None:
                deps.append(t.w)
        for t in writes:
            if t.w is not None:
                deps.append(t.w)
            deps.extend(t.r.values())
        return deps

    def _wait(self, eng, deps):
        need = {}
        for (sem, val) in deps:
            k = id(sem)
            if eng.is_pe and sem is eng.sem:
                continue
            if eng.waited.get(k, 0) >= val:
                continue
            if need.get(k, 0) < val:
                need[k] = val
        for k, val in need.items():
            eng.h.wait_ge(self.sems[k], val)
            eng.waited[k] = val

    def _commit(self, tok, key, reads, writes):
        for t in reads:
            t.r[key] = tok
        for t in writes:
            t.w = tok
            t.r = {}

    def op(self, en, fn, reads=(), writes=()):
        eng = self.E[en]
        self._wait(eng, self._deps(reads, writes))
        ins = fn(eng.h)
        eng.n += 1
        ins.then_inc(eng.sem, 1)
        self._commit((eng.sem, eng.n), en, reads, writes)

    def dma(self, q, out, in_, reads=(), writes=()):
        Q = self.Q[q]
        eng = Q["eng"]
        j = Q["i"] % len(Q["sems"])
        Q["i"] += 1
        deps = self._deps(reads, writes)
        if Q["vals"][j] > 0:
            deps.append((Q["sems"][j], Q["vals"][j]))
        self._wait(eng, deps)
        ins = eng.h.dma_start(out=out, in_=in_)
        Q["vals"][j] += 16
        ins.then_inc(Q["sems"][j], 16)
        self._commit((Q["sems"][j], Q["vals"][j]), f"q{q}{j}", reads, writes)

    def all_tokens(self):
        toks = [(e.sem, e.n) for e in self.E.values() if e.n > 0]
        for Q in self.Q.values():
            for s, v in zip(Q["sems"], Q["vals"]):
                if v > 0:
                    toks.append((s, v))
        return toks

    def barrier(self, engines=("pe", "act", "dve", "pool", "sp")):
        toks = self.all_tokens()
        for en in engines:
            self._wait(self.E[en], toks)

    def finish(self):
        self._wait(self.E["sp"], self.all_tokens())


def sb(nc, st, name, shape, dt):
    return st.enter_context(nc.sbuf_tensor(name, shape, dt))


def emit_norm_transpose(S, nc, xt, xt_tk, junk, junk_tk, ssq, ssq_tk, xn, xn_tk, psT, psT_tk,
                        ident, gT, dst_ap, dst_tk, col):
    c0 = 3 * col
    S.op("act", lambda e: e.activation(out=junk[:], in_=xt, func=AF.Square,
                                       accum_out=ssq[:, c0:c0 + 1]),
         reads=[xt_tk], writes=[junk_tk, ssq_tk])
    S.op("act", lambda e: e.activation(out=ssq[:, c0 + 1:c0 + 2], in_=ssq[:, c0:c0 + 1],
                                       func=AF.Sqrt, scale=1.0 / D, bias=EPS),
         reads=[ssq_tk], writes=[ssq_tk])
    S.op("dve", lambda e: e.reciprocal(out=ssq[:, c0 + 2:c0 + 3], in_=ssq[:, c0 + 1:c0 + 2]),
         reads=[ssq_tk], writes=[ssq_tk])
    S.op("act", lambda e: e.activation(out=xn[:], in_=xt, func=AF.Copy,
                                       scale=ssq[:, c0 + 2:c0 + 3]),
         reads=[xt_tk, ssq_tk], writes=[xn_tk])
    pv = psT[:, :].bitcast(BF16)
    for kc in range(KC):
        S.op("pe", lambda e, kc=kc: e.transpose(pv[:, kc * 128:(kc + 1) * 128],
                                                xn[:, kc * 128:(kc + 1) * 128], ident),
             reads=[xn_tk], writes=[psT_tk])
    S.op("dve", lambda e: e.tensor_tensor(
        out=dst_ap, in0=pv.rearrange("p (c t) -> p c t", c=KC),
        in1=gT.unsqueeze(2).to_broadcast([128, KC, 128]), op=ALU.mult),
         reads=[psT_tk], writes=[dst_tk])


def phase_post(nc, S, P, seq, x_res, oT_all, sel, gTf, w_o, w_g, w_u, w_d, y_out, cb16, lname):
    H = seq // 2
    NT = H // 128
    NCH = H // 512
    with contextlib.ExitStack() as st:
        xr = sb(nc, st, f"{lname}xr", [128, NT, D], F32)
        hT = sb(nc, st, f"{lname}hT", [128, KC, H], BF16)
        wo = sb(nc, st, f"{lname}wo", [128, KC, D], BF16)
        oT = sb(nc, st, f"{lname}oT", [128, KC, H], BF16)
        oT1 = sb(nc, st, f"{lname}oT1", [128, KC, 512], BF16)
        cb = sb(nc, st, f"{lname}cb", [128, 640], BF16)
        selt = sb(nc, st, f"{lname}sel", [128, 2], F32)
        gt = sb(nc, st, f"{lname}gt", [128, KC], F32)
        junk = sb(nc, st, f"{lname}junk", [128, D], BF16)
        xn = [sb(nc, st, f"{lname}xn{i}", [128, D], BF16) for i in range(2)]
        ssq = sb(nc, st, f"{lname}ssq", [128, 3 * NT], F32)
        wgu = [sb(nc, st, f"{lname}wgu{i}", [128, 2, KC, 256], BF16) for i in range(2)]
        wdn = [sb(nc, st, f"{lname}wdn{i}", [128, 2, D], BF16) for i in range(2)]
        aT = [sb(nc, st, f"{lname}aT{i}", [128, 2, H], BF16) for i in range(2)]
        sg = [sb(nc, st, f"{lname}sg{i}", [128, 512], F32) for i in range(2)]
        ps = P
        T = lambda: Tk()
        xr_tk = [T() for _ in range(NT)]
        hT_tk = [T() for _ in range(NCH)]
        wo_tk, cb_tk, sel_tk, gt_tk, junk_tk, ssq_tk = T(), T(), T(), T(), T(), T()
        oT_tk = [T() for _ in range(NCH)]
        oT1_tk = T()
        xn_tk = [T(), T()]
        wgu_tk = [T(), T()]
        wdn_tk = [T(), T()]
        aT_tk = [[T() for _ in range(NCH)] for _ in range(2)]
        sg_tk = [T(), T()]
        ps_tk = [T() for _ in range(8)]

        S.dma("sp", cb[:], cb16, writes=[cb_tk])
        S.dma("sp", selt[:], sel, writes=[sel_tk])
        S.dma("sp", gt[:], gTf, writes=[gt_tk])
        for i in range(2):
            S.dma("pool", wo[:, 4 * i:4 * i + 4, :],
                  w_o[512 * i:512 * i + 512, :].rearrange("(c p) n -> p c n", p=128), writes=[wo_tk])
        for t in range(NT):
            S.dma("sp", xr[:, t, :], x_res[t * 128:(t + 1) * 128, :], writes=[xr_tk[t]])
        ident = cb[:, C_ID:C_ID + 128]

        def load_group(gi):
            b = gi % 2
            f0 = gi * 256
            S.dma("pool", wgu[b][:, 0, :, :],
                  w_g[:, f0:f0 + 256].rearrange("(c p) n -> p c n", p=128), writes=[wgu_tk[b]])
            S.dma("pool", wgu[b][:, 1, :, :],
                  w_u[:, f0:f0 + 256].rearrange("(c p) n -> p c n", p=128), writes=[wgu_tk[b]])
            S.dma("pool", wdn[b][:, :, :],
                  w_d[f0:f0 + 256, :].rearrange("(c p) n -> p c n", p=128), writes=[wdn_tk[b]])

        for c in range(NCH):
            S.dma("sp", oT[:, :, c * 512:(c + 1) * 512],
                  oT_all[:, c * 512:(c + 1) * 512].rearrange("(k p) t -> p k t", p=128),
                  writes=[oT_tk[c]])
            S.dma("sp", oT1[:], oT_all[:, H + c * 512:H + (c + 1) * 512]
                  .rearrange("(k p) t -> p k t", p=128), writes=[oT1_tk])
            S.op("dve", lambda e, c=c: e.tensor_scalar(
                oT[:, :, c * 512:(c + 1) * 512], oT[:, :, c * 512:(c + 1) * 512],
                selt[:, 0:1], None, ALU.mult), reads=[sel_tk], writes=[oT_tk[c]])
            S.op("dve", lambda e, c=c: e.scalar_tensor_tensor(
                out=oT[:, :, c * 512:(c + 1) * 512], in0=oT1[:], scalar=selt[:, 1:2],
                in1=oT[:, :, c * 512:(c + 1) * 512], op0=ALU.mult, op1=ALU.add),
                 reads=[sel_tk, oT1_tk], writes=[oT_tk[c]])
        load_group(0)
        load_group(1)
        for t in range(NT):
            c = t // 4
            for hf in range(2):
                bank = ps[hf]
                for kc in range(KC):
                    S.op("pe", lambda e, kc=kc, hf=hf, bank=bank, t=t: e.matmul(
                        bank[:, :], lhsT=oT[:, kc, t * 128:(t + 1) * 128],
                        rhs=wo[:, kc, hf * 512:(hf + 1) * 512], start=(kc == 0), stop=(kc == KC - 1)),
                         reads=[oT_tk[c], wo_tk], writes=[ps_tk[hf]])
                S.op("dve", lambda e, hf=hf, bank=bank, t=t: e.tensor_tensor(
                    out=xr[:, t, hf * 512:(hf + 1) * 512], in0=bank[:, :],
                    in1=xr[:, t, hf * 512:(hf + 1) * 512], op=ALU.add),
                     reads=[ps_tk[hf]], writes=[xr_tk[t]])
            emit_norm_transpose(S, nc, xr[:, t, :], xr_tk[t], junk, junk_tk, ssq, ssq_tk,
                                xn[t % 2], xn_tk[t % 2], ps[2 + t % 2], ps_tk[2 + t % 2],
                                ident, gt[:], hT[:, :, t * 128:(t + 1) * 128], hT_tk[c], t)

        NG = NFF // 2
        for gi in range(NG):
            b = gi % 2
            for j in range(2):
                for c in range(NCH):
                    gb, ub = ps[(2 * (j * NCH + c)) % 4], ps[(2 * (j * NCH + c)) % 4 + 1]
                    gtk, utk = ps_tk[(2 * (j * NCH + c)) % 4], ps_tk[(2 * (j * NCH + c)) % 4 + 1]
                    for (which, bank, btk) in ((0, gb, gtk), (1, ub, utk)):
                        for kc in range(KC):
                            S.op("pe", lambda e, kc=kc, which=which, bank=bank, c=c, j=j, b=b: e.matmul(
                                bank[:, :], lhsT=wgu[b][:, which, kc, j * 128:(j + 1) * 128],
                                rhs=hT[:, kc, c * 512:(c + 1) * 512],
                                start=(kc == 0), stop=(kc == KC - 1)),
                                 reads=[wgu_tk[b], hT_tk[c]], writes=[btk])
                    k = (j * NCH + c) % 2
                    S.op("act", lambda e, gb=gb, k=k: e.activation(out=sg[k][:], in_=gb[:, :], func=AF.Silu),
                         reads=[gtk], writes=[sg_tk[k]])
                    S.op("dve", lambda e, ub=ub, k=k, b=b, j=j, c=c: e.tensor_tensor(
                        out=aT[b][:, j, c * 512:(c + 1) * 512], in0=ub[:, :], in1=sg[k][:], op=ALU.mult),
                         reads=[utk, sg_tk[k]], writes=[aT_tk[b][c]])
            for t in range(NT):
                c = t // 4
                for hf in range(2):
                    bi = 4 + (2 * t + hf) % 4
                    bank = ps[bi]
                    for j in range(2):
                        S.op("pe", lambda e, j=j, hf=hf, bank=bank, t=t, b=b: e.matmul(
                            bank[:, :], lhsT=aT[b][:, j, t * 128:(t + 1) * 128],
                            rhs=wdn[b][:, j, hf * 512:(hf + 1) * 512], start=(j == 0), stop=(j == 1)),
                             reads=[aT_tk[b][c], wdn_tk[b]], writes=[ps_tk[bi]])
                    S.op("dve", lambda e, hf=hf, bank=bank, t=t: e.tensor_tensor(
                        out=xr[:, t, hf * 512:(hf + 1) * 512], in0=bank[:, :],
                        in1=xr[:, t, hf * 512:(hf + 1) * 512], op=ALU.add),
                         reads=[ps_tk[bi]], writes=[xr_tk[t]])
                if gi == NG - 1:
                    S.dma("sp", y_out[t * 128:(t + 1) * 128, :], xr[:, t, :], reads=[xr_tk[t]])
            if gi + 2 < NG:
                load_group(gi + 2)
        S.barrier()


def make_consts():
    k = np.arange(128)
    c16 = np.zeros((128, 640), np.float32)
    c16[:, C_ID:C_ID + 128] = np.eye(128)
    c16[:, C_TRI:C_TRI + 128] = (k[None, :] >= k[:, None])
    c16[:, C_TRS:C_TRS + 128] = (k[None, :] > k[:, None])
    c16[:, C_NTI:C_NTI + 128] = -1.0 * (k[:, None] >= k[None, :])
    c16[:, C_NON:C_NON + 128] = -1.0
    f32 = np.zeros((128, F_W), np.float32)
    f32[:, F_ONE:F_ONE + 64] = 1.0
    f32[127, F_SEL:F_SEL + 128] = 1.0
    f32[:, F_TRC:F_TRC + 128] = (k[:, None] <= k[None, :])
    f32[:, F_ALL:F_ALL + 128] = 1.0
    f32[:, F_INV:F_INV + 16] = (10000.0 ** (-np.arange(16, dtype=np.float32) / 16.0))[None, :]
    return c16.astype(ml_dtypes.bfloat16), f32


def psum_banks(nc, st):
    return [st.enter_context(nc.psum_tensor(f"psb{i}", [128, 512], F32)) for i in range(8)]


def build_post(seq):
    nc = bass.Bass("TRN2", target_bir_lowering=False)
    H = seq // 2
    dt = lambda n, s, d, k="ExternalInput": nc.dram_tensor(n, s, d, kind=k).ap()
    x_res = dt("x_res", [H, D], F32)
    oT_all = dt("oT_all", [D, seq], BF16)
    sel = dt("sel", [128, 2], F32)
    gTf = dt("gTf", [128, KC], F32)
    w_o = dt("w_o", [D, D], F32)
    w_g = dt("w_g", [D, DFF], F32)
    w_u = dt("w_u", [D, DFF], F32)
    w_d = dt("w_d", [DFF, D], F32)
    cb16 = dt("cb16", [128, 640], BF16)
    y = dt("y", [H, D], F32, "ExternalOutput")
    with contextlib.ExitStack() as st:
        S = Sched(nc, st)
        P = psum_banks(nc, st)
        phase_post(nc, S, P, seq, x_res, oT_all, sel, gTf, w_o, w_g, w_u, w_d, y, cb16, "p")
        S.finish()
    return nc


class AttnCtx:
    pass


def attn_alloc(nc, S, st, P, ps_tk, lname, sbanks=(0, 1, 2)):
    A = AttnCtx()
    A.Pt = [sb(nc, st, f"{lname}P{i}", [128, 512], BF16) for i in range(4)]
    A.Pt_tk = [Tk() for _ in range(4)]
    A.rrow = sb(nc, st, f"{lname}rrow", [128, 512], F32)
    A.rrow_tk = Tk()
    A.bcs = sb(nc, st, f"{lname}bcs", [64, 512], F32)
    A.bcs_tk = Tk()
    A.osb = [sb(nc, st, f"{lname}osb{i}", [64, 512], BF16) for i in range(2)]
    A.osb_tk = [Tk(), Tk()]
    A.Sb = [P[i] for i in sbanks]
    A.Sb_tk = [ps_tk[i] for i in sbanks]
    A.Ob = [P[3], P[4]]
    A.Ob_tk = [ps_tk[3], ps_tk[4]]
    A.bc = P[5]
    A.bc_tk = ps_tk[5]
    A.pi = 0
    A.si = 0
    A.ui = 0
    return A


def softmax_unit(nc, S, A, c, kT_ap, qT_ap, kq_tk, v_ap, v_tk, scale, bias_fn, bias_tk,
                 tri, ones_f, c_tk, oT_dst_rows):
    nkt = 4 * c + 4
    u = A.ui
    A.ui += 1
    Ob, Ob_tk = A.Ob[u % 2], A.Ob_tk[u % 2]
    steps = []
    for kt in range(nkt):
        j = kt - 4 * c
        c0 = 128 * j if j > 0 else 0
        steps.append((kt, j, c0))

    def emit_s(i):
        kt, j, c0 = steps[i]
        si = A.si
        A.si += 1
        bank, btk = A.Sb[si % 3], A.Sb_tk[si % 3]
        S.op("pe", lambda e: e.matmul(bank[:, c0:512], lhsT=kT_ap(kt), rhs=qT_ap(c0, 512),
                                      start=True, stop=True),
             reads=list(kq_tk(kt)), writes=[btk])
        return bank, btk

    pend = []
    LOOK = 2
    for i in range(min(LOOK, nkt)):
        pend.append(emit_s(i))
    for i in range(nkt):
        kt, j, c0 = steps[i]
        if i + LOOK < nkt:
            pend.append(emit_s(i + LOOK))
        bank, btk = pend.pop(0)
        pi = A.pi
        A.pi += 1
        Pt, Ptk = A.Pt[pi % 4], A.Pt_tk[pi % 4]
        b = bias_fn(kt)
        rd = [btk] + ([bias_tk] if b is not None else [])
        if b is not None:
            S.op("act", lambda e: e.activation(out=Pt[:, c0:512], in_=bank[:, c0:512], func=AF.Exp,
                                               scale=scale, bias=b), reads=rd, writes=[Ptk])
        else:
            S.op("act", lambda e: e.activation(out=Pt[:, c0:512], in_=bank[:, c0:512], func=AF.Exp,
                                               scale=scale), reads=rd, writes=[Ptk])
        if j >= 0:
            S.op("dve", lambda e: e.tensor_tensor(out=Pt[:, c0:c0 + 128], in0=Pt[:, c0:c0 + 128],
                                                  in1=tri, op=ALU.mult), reads=[c_tk], writes=[Ptk])
        S.op("pe", lambda e: e.matmul(Ob[0:65, c0:512], lhsT=v_ap(kt), rhs=Pt[:, c0:512],
                                      start=(i == 0), stop=(i == nkt - 1)),
             reads=[Ptk, v_tk(kt)], writes=[Ob_tk])
    attn_epilogue(nc, S, A, u, 64, ones_f, c_tk, oT_dst_rows, c)


def attn_epilogue(nc, S, A, u, drow, ones_f, c_tk, oT_dst_rows, c):
    Ob, Ob_tk = A.Ob[u % 2], A.Ob_tk[u % 2]
    osb, osb_tk = A.osb[u % 2], A.osb_tk[u % 2]
    if drow is not None:
        S.op("dve", lambda e: e.reciprocal(out=A.rrow[64:65, :], in_=Ob[64:65, :]),
             reads=[Ob_tk], writes=[A.rrow_tk])
        S.op("pe", lambda e: e.matmul(A.bc[0:64, :], lhsT=ones_f[64:65, 0:64], rhs=A.rrow[64:65, :],
                                      start=True, stop=True),
             reads=[A.rrow_tk, c_tk], writes=[A.bc_tk])
        S.op("act", lambda e: e.activation(out=A.bcs[:, :], in_=A.bc[0:64, :], func=AF.Copy),
             reads=[A.bc_tk], writes=[A.bcs_tk])
        S.op("dve", lambda e: e.tensor_tensor(out=osb[:, :], in0=Ob[0:64, :], in1=A.bcs[:, :], op=ALU.mult),
             reads=[Ob_tk, A.bcs_tk], writes=[osb_tk])
    else:
        S.op("dve", lambda e: e.tensor_copy(out=osb[:, :], in_=Ob[0:64, :]), reads=[Ob_tk], writes=[osb_tk])
    S.dma("sp", oT_dst_rows[:, c * 512:(c + 1) * 512], osb[:, :], reads=[osb_tk])


def emit_rstd(S, ssum_ap, tmp_ap, out_ap, n, tk):
    S.op("act", lambda e: e.activation(out=tmp_ap, in_=ssum_ap, func=AF.Sqrt, scale=1.0 / n, bias=EPS),
         reads=[tk], writes=[tk])
    S.op("dve", lambda e: e.reciprocal(out=out_ap, in_=tmp_ap), reads=[tk], writes=[tk])


def phase_attn1(nc, S, P, seq, x_src, w1, gT1, fb, gqk, cb16, cf32, oT_dst, lname):
    NT = seq // 128
    NCH = seq // 512
    NH = 8
    with contextlib.ExitStack() as st:
        wq = sb(nc, st, f"{lname}wq", [128, KC, 1544], BF16)
        hT = [sb(nc, st, f"{lname}hT{i}", [128, KC, 512], BF16) for i in range(2)]
        xt = [sb(nc, st, f"{lname}xt{i}", [128, D], F32) for i in range(3)]
        qT = sb(nc, st, f"{lname}qT", [128, 4, seq], BF16)
        kT = sb(nc, st, f"{lname}kT", [128, 4, seq], BF16)
        Va = sb(nc, st, f"{lname}Va", [128, NT, NH, 65], BF16)
        cb = sb(nc, st, f"{lname}cb", [128, 640], BF16)
        cf = sb(nc, st, f"{lname}cf", [128, F_W], F32)
        gt = sb(nc, st, f"{lname}gt", [128, KC], F32)
        fbt = sb(nc, st, f"{lname}fbt", [128, NH], F32)
        gq = sb(nc, st, f"{lname}gq", [128, 128], F32)
        junk = sb(nc, st, f"{lname}junk", [128, D], BF16)
        xn = [sb(nc, st, f"{lname}xn{i}", [128, D], BF16) for i in range(2)]
        ssq = sb(nc, st, f"{lname}ssq", [128, 3 * NT], F32)
        sq = sb(nc, st, f"{lname}sq", [128, 512], F32)
        qn = sb(nc, st, f"{lname}qn", [128, 512], F32)
        qb = sb(nc, st, f"{lname}qb", [128, 512], BF16)
        st8 = sb(nc, st, f"{lname}st8", [128, 3, NH], F32)
        spl = sb(nc, st, f"{lname}spl", [128, NT, NH], F32)
        fe = sb(nc, st, f"{lname}fe", [128, 2, NH], F32)
        cn = sb(nc, st, f"{lname}cn", [128, NT, NH], F32)
        car = sb(nc, st, f"{lname}car", [128, NT, NH], F32)
        tot = sb(nc, st, f"{lname}tot", [128, NT, NH], F32)
        cref = sb(nc, st, f"{lname}cref", [128, NT, NH], F32)
        nb = [sb(nc, st, f"{lname}nb{i}", [128, NT], F32) for i in range(2)]
        T = lambda: Tk()
        wq_tk, cb_tk, cf_tk, sm_tk, junk_tk, ssq_tk = T(), T(), T(), T(), T(), T()
        hT_tk = [T(), T()]
        xt_tk = [T(), T(), T()]
        xn_tk = [T(), T()]
        qk_tk = [T() for _ in range(NCH)]
        Va_tk = [T() for _ in range(NCH)]
        sq_tk, qn_tk, qb_tk, st8_tk, spl_tk, fe_tk, cn_tk, car_tk, tot_tk, cref_tk = (T() for _ in range(10))
        nb_tk = [T(), T()]
        ps_tk = [T() for _ in range(8)]
        ps = P

        S.dma("sp", cb[:], cb16, writes=[cb_tk])
        S.dma("sp", cf[:], cf32, writes=[cf_tk])
        S.dma("sp", gt[:], gT1, writes=[sm_tk])
        S.dma("sp", fbt[:], fb, writes=[sm_tk])
        S.dma("sp", gq[:], gqk, writes=[sm_tk])
        for i in range(2):
            S.dma("pool", wq[:, 4 * i:4 * i + 4, :],
                  w1[512 * i:512 * i + 512, :].rearrange("(c p) n -> p c n", p=128), writes=[wq_tk])
        ident = cb[:, C_ID:C_ID + 128]
        S.op("pool", lambda e: e.memset(Va[:, :, :, 64:65], 1.0), writes=Va_tk)

        def qk_norm(bank, btk, goff, dstT, t, tbank, ttk):
            c = t // 4
            S.op("act", lambda e: e.activation(out=sq[:], in_=bank[:, :], func=AF.Square),
                 reads=[btk], writes=[sq_tk])
            S.op("dve", lambda e: e.tensor_reduce(out=st8[:, 0, :], in_=sq[:].rearrange("p (h d) -> p h d", h=NH),
                                                  axis=AX.X, op=ALU.add), reads=[sq_tk], writes=[st8_tk])
            emit_rstd(S, st8[:, 0, :], st8[:, 1, :], st8[:, 2, :], 64, st8_tk)
            S.op("dve", lambda e: e.tensor_tensor(
                out=qn[:].rearrange("p (h d) -> p h d", h=NH), in0=bank[:, :].rearrange("p (h d) -> p h d", h=NH),
                in1=st8[:, 2, :].unsqueeze(2).to_broadcast([128, NH, 64]), op=ALU.mult),
                 reads=[btk, st8_tk], writes=[qn_tk])
            S.op("pool", lambda e: e.tensor_tensor(
                out=qb[:].rearrange("p (h d) -> p h d", h=NH), in0=qn[:].rearrange("p (h d) -> p h d", h=NH),
                in1=gq[:, goff:goff + 64].unsqueeze(1).to_broadcast([128, NH, 64]), op=ALU.mult),
                 reads=[qn_tk, sm_tk], writes=[qb_tk])
            tv = tbank[:, :].bitcast(BF16)
            for hp in range(4):
                S.op("pe", lambda e, hp=hp: e.transpose(tv[:, hp * 128:(hp + 1) * 128],
                                                        qb[:, hp * 128:(hp + 1) * 128], ident),
                     reads=[qb_tk, cb_tk], writes=[ttk])
            S.op("act", lambda e: e.activation(out=dstT[:, :, t * 128:(t + 1) * 128],
                                               in_=tv[:, 0:512].rearrange("p (h t) -> p h t", h=4), func=AF.Copy),
                 reads=[ttk], writes=[qk_tk[c]])

        for t in range(NT):
            c = t // 4
            xb = t % 3
            S.dma("sp", xt[xb][:], x_src[t * 128:(t + 1) * 128, :], writes=[xt_tk[xb]])
            emit_norm_transpose(S, nc, xt[xb][:], xt_tk[xb], junk, junk_tk, ssq, ssq_tk,
                                xn[t % 2], xn_tk[t % 2], ps[4 + t % 2], ps_tk[4 + t % 2],
                                ident, gt[:], hT[c % 2][:, :, (t % 4) * 128:(t % 4 + 1) * 128], hT_tk[c % 2], t)
            if t % 4 != 3:
                continue
            for tt in range(4 * c, 4 * c + 4):
                lo = (tt % 4) * 128
                for (bi, w0, w1_) in ((0, 0, 512), (1, 512, 1024), (2, 1024, 1536), (3, 1536, 1544)):
                    for kc in range(KC):
                        S.op("pe", lambda e, kc=kc, bi=bi, w0=w0, w1_=w1_, lo=lo: e.matmul(
                            ps[bi][:, 0:w1_ - w0], lhsT=hT[c % 2][:, kc, lo:lo + 128], rhs=wq[:, kc, w0:w1_],
                            start=(kc == 0), stop=(kc == KC - 1)),
                             reads=[hT_tk[c % 2], wq_tk], writes=[ps_tk[bi]])
                qk_norm(ps[0], ps_tk[0], 0, qT, tt, ps[6], ps_tk[6])
                qk_norm(ps[1], ps_tk[1], 64, kT, tt, ps[7], ps_tk[7])
                S.op("act", lambda e, tt=tt: e.activation(
                    out=Va[:, tt, :, 0:64], in_=ps[2][:, :].rearrange("p (h d) -> p h d", h=NH), func=AF.Copy),
                     reads=[ps_tk[2]], writes=[Va_tk[c]])
                S.op("dve", lambda e: e.tensor_tensor(out=fe[:, 0, :], in0=ps[3][:, 0:NH], in1=fbt[:], op=ALU.add),
                     reads=[ps_tk[3], sm_tk], writes=[fe_tk])
                S.op("act", lambda e: e.activation(out=fe[:, 1, :], in_=fe[:, 0, :], func=AF.Exp, scale=-1.0),
                     reads=[fe_tk], writes=[fe_tk])
                S.op("act", lambda e, tt=tt: e.activation(out=spl[:, tt, :], in_=fe[:, 1, :], func=AF.Ln, bias=1.0),
                     reads=[fe_tk], writes=[spl_tk])

        NW = NT * NH
        spf = spl[:].rearrange("p t h -> p (t h)")
        S.op("pe", lambda e: e.matmul(ps[0][:, 0:NW], lhsT=cf[:, F_TRC:F_TRC + 128], rhs=spf, start=True, stop=True),
             reads=[spl_tk, cf_tk], writes=[ps_tk[0]])
        S.op("pe", lambda e: e.matmul(ps[1][:, 0:NW], lhsT=cf[:, F_ALL:F_ALL + 128], rhs=spf, start=True, stop=True),
             reads=[spl_tk, cf_tk], writes=[ps_tk[1]])
        S.op("act", lambda e: e.activation(out=tot[:].rearrange("p t h -> p (t h)"), in_=ps[1][:, 0:NW], func=AF.Copy),
             reads=[ps_tk[1]], writes=[tot_tk])
        S.op("dve", lambda e: e.memset(car[:, 0, :], 0.0), writes=[car_tk])
        for t in range(1, NT):
            S.op("dve", lambda e, t=t: e.tensor_tensor(out=car[:, t, :], in0=car[:, t - 1, :], in1=tot[:, t - 1, :],
                                                       op=ALU.add), reads=[tot_tk, car_tk], writes=[car_tk])
        S.op("dve", lambda e: e.tensor_tensor(out=cn[:].rearrange("p t h -> p (t h)"), in0=ps[0][:, 0:NW],
                                              in1=car[:].rearrange("p t h -> p (t h)"), op=ALU.add),
             reads=[ps_tk[0], car_tk], writes=[cn_tk])
        S.op("pe", lambda e: e.matmul(ps[2][:, 0:NW], lhsT=cf[:, F_SEL:F_SEL + 128],
                                      rhs=cn[:].rearrange("p t h -> p (t h)"), start=True, stop=True),
             reads=[cn_tk, cf_tk], writes=[ps_tk[2]])
        S.op("act", lambda e: e.activation(out=cref[:].rearrange("p t h -> p (t h)"), in_=ps[2][:, 0:NW], func=AF.Copy),
             reads=[ps_tk[2]], writes=[cref_tk])
        S.barrier(("pe", "act", "dve"))

        A = attn_alloc(nc, S, st, P, ps_tk, lname)
        tri = cb[:, C_TRI:C_TRI + 128]
        for h in range(NH):
            hp, ro = h // 2, 64 * (h % 2)
            for c in range(NCH):
                nkt = 4 * c + 4
                nbt, nbk = nb[(h * NCH + c) % 2], nb_tk[(h * NCH + c) % 2]
                S.op("dve", lambda e: e.tensor_scalar(nbt[:, 0:nkt], cn[:, 0:nkt, h], cref[:, 4 * c + 1, h:h + 1],
                                                      None, ALU.subtract), reads=[cn_tk, cref_tk], writes=[nbk])
                softmax_unit(
                    nc, S, A, c,
                    kT_ap=lambda kt: kT[ro:ro + 64, hp, kt * 128:(kt + 1) * 128],
                    qT_ap=lambda a, b: qT[ro:ro + 64, hp, c * 512 + a:c * 512 + b],
                    kq_tk=lambda kt: (qk_tk[c], qk_tk[kt // 4]),
                    v_ap=lambda kt: Va[:, kt, h, :],
                    v_tk=lambda kt: Va_tk[kt // 4],
                    scale=0.125, bias_fn=lambda kt: nbt[:, kt:kt + 1], bias_tk=nbk,
                    tri=tri, ones_f=cf[:, F_ONE:F_ONE + 64], c_tk=cb_tk,
                    oT_dst_rows=oT_dst[h * 64:(h + 1) * 64, :])
        S.barrier()


def build_attn1(seq):
    nc = bass.Bass("TRN2", target_bir_lowering=False)
    dt = lambda n, s, d, k="ExternalInput": nc.dram_tensor(n, s, d, kind=k).ap()
    x_src = dt("x_src", [seq, D], F32)
    w1 = dt("w1", [D, 1544], F32)
    gT1 = dt("gT1", [128, KC], F32)
    fb = dt("fb", [128, 8], F32)
    gqk = dt("gqk", [128, 128], F32)
    cb16 = dt("cb16", [128, 640], BF16)
    cf32 = dt("cf32", [128, F_W], F32)
    oT = dt("oT", [512, seq], BF16, "ExternalOutput")
    with contextlib.ExitStack() as st:
        S = Sched(nc, st)
        P = psum_banks(nc, st)
        phase_attn1(nc, S, P, seq, x_src, w1, gT1, fb, gqk, cb16, cf32, oT, "a")
        S.finish()
    return nc


W_CQ, W_CKV, W_KR, W_SQ, W_SK, W_SV, W0_N = 0, 256, 384, 416, 672, 928, 1184
TWO_PI = 2.0 * np.pi


def sb_unit(nc, S, A, X, c, skT, sqT, kq_tk, sv, sv_tk, h, cb, cb_tk, oT_dst_rows):
    hp, ro = h // 2, 64 * (h % 2)
    nkt = 4 * c + 4
    u = A.ui
    A.ui += 1
    Ob, Ob_tk = A.Ob[u % 2], A.Ob_tk[u % 2]
    trs = cb[:, C_TRS:C_TRS + 128]
    nti = cb[:, C_NTI:C_NTI + 128]
    non = cb[:, C_NON:C_NON + 128]
    nzb = len(A.Sb)
    S.op("pe", lambda e: e.matmul(Ob[0:64, :], lhsT=X.zero[:, :], rhs=cb[:, 0:512], start=True, stop=False),
         reads=[cb_tk, X.zero_tk], writes=[Ob_tk])
    for k in range(2):
        S.op("pool", lambda e, k=k: e.memset(X.cacc[k][:, :], 0.0), writes=[X.cacc_tk[k]])
    order = list(range(nkt - 1, -1, -1))
    st = {}

    def stage_a(i):
        kt = order[i]
        j = kt - 4 * c
        c0 = 128 * j if j > 0 else 0
        si = A.si
        A.si += 1
        bank, btk = A.Sb[si % nzb], A.Sb_tk[si % nzb]
        S.op("pe", lambda e: e.matmul(bank[:, c0:512], lhsT=skT[ro:ro + 64, hp, kt * 128:(kt + 1) * 128],
                                      rhs=sqT[ro:ro + 64, hp, c0:512], start=True, stop=False),
             reads=list(kq_tk(kt)), writes=[btk])
        ei = X.ei
        X.ei += 1
        E, Etk = X.E[ei % 2], X.E_tk[ei % 2]
        Pm, Ptk = X.Pm[ei % 4], X.Pm_tk[ei % 4]
        S.op("act", lambda e: e.activation(out=E[:, c0:512], in_=bank[:, c0:512], func=AF.Exp),
             reads=[btk], writes=[Etk])
        S.op("act", lambda e: e.activation(out=Pm[:, c0:512], in_=E[:, c0:512], func=AF.Ln, bias=1.0),
             reads=[Etk], writes=[Ptk])
        if j >= 0:
            S.op("dve", lambda e: e.tensor_tensor(out=Pm[:, c0:c0 + 128], in0=Pm[:, c0:c0 + 128], in1=trs,
                                                  op=ALU.mult), reads=[cb_tk], writes=[Ptk])
        st[i] = (kt, j, c0, bank, btk, Pm, Ptk)

    def stage_b(i):
        kt, j, c0, bank, btk, Pm, Ptk = st.pop(i)
        cur, nxt = X.cacc[i % 2], X.cacc[(i + 1) % 2]
        ctk, ntk = X.cacc_tk[i % 2], X.cacc_tk[(i + 1) % 2]
        S.op("pe", lambda e: e.matmul(bank[:, c0:512], lhsT=nti, rhs=Pm[:, c0:512], start=False, stop=(i == 0)),
             reads=[Ptk, cb_tk], writes=[btk])
        if i > 0:
            S.op("pe", lambda e: e.matmul(bank[:, c0:512], lhsT=non, rhs=cur[:, c0:512], start=False, stop=True),
                 reads=[ctk, cb_tk], writes=[btk])
        ai = X.ai
        X.ai += 1
        Am, Atk = X.Am[ai % 3], X.Am_tk[ai % 3]
        S.op("act", lambda e: e.activation(out=Am[:, c0:512], in_=bank[:, c0:512], func=AF.Exp),
             reads=[btk], writes=[Atk])
        if j >= 0:
            S.op("dve", lambda e: e.tensor_tensor(out=Am[:, c0:c0 + 128], in0=Am[:, c0:c0 + 128], in1=trs,
                                                  op=ALU.mult), reads=[cb_tk], writes=[Atk])
        S.op("pe", lambda e: e.matmul(Ob[0:64, c0:512], lhsT=sv[:, kt, h * 64:(h + 1) * 64], rhs=Am[:, c0:512],
                                      start=False, stop=(i == nkt - 1)),
             reads=[Atk, sv_tk(kt)], writes=[Ob_tk])
        if i < nkt - 1:
            S.op("pool", lambda e: e.tensor_tensor(out=nxt[:, c0:512], in0=cur[:, c0:512], in1=Pm[:, c0:512],
                                                   op=ALU.add), reads=[ctk, Ptk], writes=[ntk])

    LOOK = nzb - 1
    for i in range(min(LOOK, nkt)):
        stage_a(i)
    for i in range(nkt):
        if i + LOOK < nkt:
            stage_a(i + LOOK)
        stage_b(i)
    attn_epilogue(nc, S, A, u, None, None, cb_tk, oT_dst_rows, c)


def phase_attn0(nc, S, P, seq, x_src, pos_tm, w0, wqu, wkvu, gT0, gA, gqk0, cb16, cf32, oT_dst, lname):
    NT = seq // 128
    NCH = seq // 512
    with contextlib.ExitStack() as st:
        wi = sb(nc, st, f"{lname}wi", [128, KC, W0_N], BF16)
        wq = sb(nc, st, f"{lname}wqu", [128, 2, 384], BF16)
        wkv = sb(nc, st, f"{lname}wkv", [128, 512], BF16)
        hT = sb(nc, st, f"{lname}hT", [128, KC, 512], BF16)
        xt = [sb(nc, st, f"{lname}xt{i}", [128, D], F32) for i in range(2)]
        junk = sb(nc, st, f"{lname}junk", [128, D], BF16)
        xn = [sb(nc, st, f"{lname}xn{i}", [128, D], BF16) for i in range(2)]
        ssq = sb(nc, st, f"{lname}ssq", [128, 3 * NT], F32)
        cb = sb(nc, st, f"{lname}cb", [128, 640], BF16)
        cf = sb(nc, st, f"{lname}cf", [128, F_W], F32)
        gt = sb(nc, st, f"{lname}gt", [128, KC], F32)
        gat = sb(nc, st, f"{lname}gat", [128, 3], F32)
        gq = sb(nc, st, f"{lname}gq", [128, 192], F32)
        post = sb(nc, st, f"{lname}post", [128, NT], I32)
        posf = sb(nc, st, f"{lname}posf", [128, NT], F32)
        ang = sb(nc, st, f"{lname}ang", [128, NT, 16], F32)
        rr = sb(nc, st, f"{lname}rr", [128, NT, 16], F32)
        ki = sb(nc, st, f"{lname}ki", [128, NT, 16], I32)
        cosT = sb(nc, st, f"{lname}cosT", [128, NT, 16], F32)
        sinT = sb(nc, st, f"{lname}sinT", [128, NT, 16], F32)
        kT = sb(nc, st, f"{lname}kT", [128, 4, seq], BF16)
        Va = sb(nc, st, f"{lname}Va", [128, NT, 4, 65], BF16)
        skT = sb(nc, st, f"{lname}skT", [128, 2, seq], BF16)
        sv = sb(nc, st, f"{lname}sv", [128, NT, 256], BF16)
        qT = sb(nc, st, f"{lname}qT", [128, 4, 512], BF16)
        sqT = sb(nc, st, f"{lname}sqT", [128, 2, 512], BF16)
        stt = sb(nc, st, f"{lname}stt", [128, 8], F32)
        st4 = sb(nc, st, f"{lname}st4", [128, 6, 4], F32)
        cbf = sb(nc, st, f"{lname}cbf", [128, 384], BF16)
        krs = sb(nc, st, f"{lname}krs", [128, 32], F32)
        cT = sb(nc, st, f"{lname}cT", [128, 3, 128], BF16)
        sq = sb(nc, st, f"{lname}sq", [128, 384], F32)
        qn = sb(nc, st, f"{lname}qn", [128, 4, 96], F32)
        qg = sb(nc, st, f"{lname}qg", [128, 4, 96], F32)
        rt = sb(nc, st, f"{lname}rt", [128, 4, 4, 16], F32)
        qf = sb(nc, st, f"{lname}qf", [128, 4, 96], BF16)
        X = AttnCtx()
        X.zero = sb(nc, st, f"{lname}zero", [128, 64], BF16)
        X.cacc = [sb(nc, st, f"{lname}cacc{i}", [128, 512], BF16) for i in range(2)]
        X.E = [sb(nc, st, f"{lname}E{i}", [128, 512], F32) for i in range(2)]
        X.Pm = [sb(nc, st, f"{lname}Pm{i}", [128, 512], BF16) for i in range(4)]
        X.Am = [sb(nc, st, f"{lname}Am{i}", [128, 512], BF16) for i in range(3)]
        T = lambda: Tk()
        X.zero_tk = T()
        X.cacc_tk = [T(), T()]
        X.E_tk = [T(), T()]
        X.Pm_tk = [T(), T(), T(), T()]
        X.Am_tk = [T(), T(), T()]
        X.ei = X.ai = 0
        wi_tk, cb_tk, cf_tk, sm_tk, junk_tk, ssq_tk, hT_tk, rope_tk = (T() for _ in range(8))
        xt_tk = [T(), T()]
        xn_tk = [T(), T()]
        kT_tk = [T() for _ in range(NCH)]
        Va_tk = [T() for _ in range(NCH)]
        skT_tk = [T() for _ in range(NCH)]
        sv_tk = [T() for _ in range(NCH)]
        qT_tk, sqT_tk = T(), T()
        stt_tk, st4_tk, cbf_tk, krs_tk, cT_tk, sq_tk, qn_tk, qg_tk, rt_tk, qf_tk = (T() for _ in range(10))
        ps_tk = [T() for _ in range(8)]
        ps = P

        S.dma("sp", cb[:], cb16, writes=[cb_tk])
        S.dma("sp", cf[:], cf32, writes=[cf_tk])
        S.dma("sp", gt[:], gT0, writes=[sm_tk])
        S.dma("sp", gat[:], gA, writes=[sm_tk])
        S.dma("sp", gq[:], gqk0, writes=[sm_tk])
        S.dma("sp", post[:], pos_tm, writes=[rope_tk])
        for i in range(2):
            S.dma("pool", wi[:, 4 * i:4 * i + 4, :],
                  w0[512 * i:512 * i + 512, :].rearrange("(c p) n -> p c n", p=128), writes=[wi_tk])
        S.dma("pool", wq[:], wqu.rearrange("(c p) n -> p c n", p=128), writes=[wi_tk])
        S.dma("pool", wkv[:], wkvu, writes=[wi_tk])
        ident = cb[:, C_ID:C_ID + 128]
        S.op("pool", lambda e: e.memset(Va[:, :, :, 64:65], 1.0), writes=Va_tk)
        S.op("pool", lambda e: e.memset(X.zero[:, :], 0.0), writes=[X.zero_tk])

        S.op("dve", lambda e: e.tensor_copy(out=posf[:], in_=post[:]), reads=[rope_tk], writes=[rope_tk])
        S.op("dve", lambda e: e.tensor_tensor(
            out=ang[:], in0=posf[:].unsqueeze(2).to_broadcast([128, NT, 16]),
            in1=cf[:, F_INV:F_INV + 16].unsqueeze(1).to_broadcast([128, NT, 16]), op=ALU.mult),
             reads=[rope_tk, cf_tk], writes=[rope_tk])
        for (dst, shift) in ((sinT, 0.0), (cosT, 0.25)):
            S.op("dve", lambda e, shift=shift: e.tensor_scalar(rr[:], ang[:], 1.0 / TWO_PI, shift, ALU.mult, ALU.add),
                 reads=[rope_tk], writes=[rope_tk])
            S.op("dve", lambda e: e.tensor_copy(out=ki[:], in_=rr[:]), reads=[rope_tk], writes=[rope_tk])
            S.op("dve", lambda e: e.tensor_copy(out=rr[:], in_=ki[:]), reads=[rope_tk], writes=[rope_tk])
            S.op("dve", lambda e: e.scalar_tensor_tensor(out=rr[:], in0=rr[:], scalar=-TWO_PI, in1=ang[:],
                                                         op0=ALU.mult, op1=ALU.add),
                 reads=[rope_tk], writes=[rope_tk])
            S.op("dve", lambda e, shift=shift: e.tensor_scalar(rr[:], rr[:], shift * TWO_PI, np.pi, ALU.add, ALU.min),
                 reads=[rope_tk], writes=[rope_tk])
            S.op("dve", lambda e: e.tensor_scalar(rr[:], rr[:], -np.pi, None, ALU.max),
                 reads=[rope_tk], writes=[rope_tk])
            S.op("act", lambda e, dst=dst: e.activation(out=dst[:], in_=rr[:], func=AF.Sin),
                 reads=[rope_tk], writes=[rope_tk])

        def rope_and_T(src, tt, dstT, dst_lo, dst_tk):
            cosb = cosT[:, tt, :].unsqueeze(1).to_broadcast([128, 4, 16])
            sinb = sinT[:, tt, :].unsqueeze(1).to_broadcast([128, 4, 16])
            r1, r2 = src[:, :, 64:80], src[:, :, 80:96]
            S.op("dve", lambda e: e.tensor_tensor(out=rt[:, 0], in0=r1, in1=cosb, op=ALU.mult),
                 reads=[qg_tk, rope_tk], writes=[rt_tk])
            S.op("pool", lambda e: e.tensor_tensor(out=rt[:, 1], in0=r2, in1=sinb, op=ALU.mult),
                 reads=[qg_tk, rope_tk], writes=[rt_tk])
            S.op("dve", lambda e: e.tensor_tensor(out=rt[:, 2], in0=r2, in1=cosb, op=ALU.mult),
                 reads=[qg_tk, rope_tk], writes=[rt_tk])
            S.op("pool", lambda e: e.tensor_tensor(out=rt[:, 3], in0=r1, in1=sinb, op=ALU.mult),
                 reads=[qg_tk, rope_tk], writes=[rt_tk])
            S.op("dve", lambda e: e.tensor_tensor(out=qf[:, :, 64:80], in0=rt[:, 0], in1=rt[:, 1], op=ALU.subtract),
                 reads=[rt_tk], writes=[qf_tk])
            S.op("dve", lambda e: e.tensor_tensor(out=qf[:, :, 80:96], in0=rt[:, 2], in1=rt[:, 3], op=ALU.add),
                 reads=[rt_tk], writes=[qf_tk])
            S.op("act", lambda e: e.activation(out=qf[:, :, 0:64], in_=src[:, :, 0:64], func=AF.Copy),
                 reads=[qg_tk], writes=[qf_tk])
            tv = ps[4][:, :].bitcast(BF16)
            for h in range(4):
                S.op("pe", lambda e, h=h: e.transpose(tv[0:96, h * 128:(h + 1) * 128], qf[:, h, :], ident),
                     reads=[qf_tk, cb_tk], writes=[ps_tk[4]])
            S.op("act", lambda e: e.activation(out=dstT[0:96, :, dst_lo:dst_lo + 128],
                                               in_=tv[0:96, 0:512].rearrange("p (h t) -> p h t", h=4), func=AF.Copy),
                 reads=[ps_tk[4]], writes=[dst_tk])

        def prep_chunk(c):
            for tt in range(4 * c, 4 * c + 4):
                xb = tt % 2
                S.dma("sp", xt[xb][:], x_src[tt * 128:(tt + 1) * 128, :], writes=[xt_tk[xb]])
                emit_norm_transpose(S, nc, xt[xb][:], xt_tk[xb], junk, junk_tk, ssq, ssq_tk,
                                    xn[tt % 2], xn_tk[tt % 2], ps[6 + tt % 2], ps_tk[6 + tt % 2],
                                    ident, gt[:], hT[:, :, (tt % 4) * 128:(tt % 4 + 1) * 128], hT_tk, tt)
            for mt in range(2):
                for (bi, woff) in ((0, W_SQ), (1, W_SK)):
                    for kc in range(KC):
                        S.op("pe", lambda e, kc=kc, bi=bi, woff=woff, mt=mt: e.matmul(
                            ps[bi][:, :], lhsT=wi[:, kc, woff + mt * 128:woff + (mt + 1) * 128], rhs=hT[:, kc, :],
                            start=(kc == 0), stop=(kc == KC - 1)), reads=[wi_tk, hT_tk], writes=[ps_tk[bi]])
                S.op("act", lambda e, mt=mt: e.activation(out=sqT[:, mt, :], in_=ps[0][:, :], func=AF.Copy, scale=0.125),
                     reads=[ps_tk[0]], writes=[sqT_tk])
                S.op("dve", lambda e, mt=mt: e.tensor_copy(out=skT[:, mt, c * 512:(c + 1) * 512], in_=ps[1][:, :]),
                     reads=[ps_tk[1]], writes=[skT_tk[c]])
            for tt in range(4 * c, 4 * c + 4):
                lo = (tt % 4) * 128
                for kc in range(KC):
                    S.op("pe", lambda e, kc=kc: e.matmul(ps[2][:, 0:256], lhsT=hT[:, kc, lo:lo + 128],
                                                         rhs=wi[:, kc, W_SV:W_SV + 256], start=(kc == 0), stop=(kc == KC - 1)),
                         reads=[wi_tk, hT_tk], writes=[ps_tk[2]])
                S.op("act", lambda e: e.activation(out=sv[:, tt, :], in_=ps[2][:, 0:256], func=AF.Copy),
                     reads=[ps_tk[2]], writes=[sv_tk[c]])
                for kc in range(KC):
                    S.op("pe", lambda e, kc=kc: e.matmul(ps[3][:, 0:416], lhsT=hT[:, kc, lo:lo + 128],
                                                         rhs=wi[:, kc, 0:416], start=(kc == 0), stop=(kc == KC - 1)),
                         reads=[wi_tk, hT_tk], writes=[ps_tk[3]])
                m = ps[3]
                S.op("act", lambda e: e.activation(out=junk[:, 0:256], in_=m[:, 0:256], func=AF.Square,
                                                   accum_out=stt[:, 0:1]), reads=[ps_tk[3]], writes=[junk_tk, stt_tk])
                S.op("act", lambda e: e.activation(out=junk[:, 256:384], in_=m[:, 256:384], func=AF.Square,
                                                   accum_out=stt[:, 1:2]), reads=[ps_tk[3]], writes=[junk_tk, stt_tk])
                emit_rstd(S, stt[:, 0:1], stt[:, 2:3], stt[:, 4:5], 256, stt_tk)
                emit_rstd(S, stt[:, 1:2], stt[:, 3:4], stt[:, 5:6], 128, stt_tk)
                S.op("act", lambda e: e.activation(out=cbf[:, 0:256], in_=m[:, 0:256], func=AF.Copy, scale=stt[:, 4:5]),
                     reads=[ps_tk[3], stt_tk], writes=[cbf_tk])
                S.op("act", lambda e: e.activation(out=cbf[:, 256:384], in_=m[:, 256:384], func=AF.Copy, scale=stt[:, 5:6]),
                     reads=[ps_tk[3], stt_tk], writes=[cbf_tk])
                S.op("dve", lambda e: e.tensor_copy(out=krs[:], in_=m[:, 384:416]), reads=[ps_tk[3]], writes=[krs_tk])
                tv = ps[4][:, :].bitcast(BF16)
                for i in range(3):
                    S.op("pe", lambda e, i=i: e.transpose(tv[:, i * 128:(i + 1) * 128], cbf[:, i * 128:(i + 1) * 128], ident),
                         reads=[cbf_tk, cb_tk], writes=[ps_tk[4]])
                S.op("dve", lambda e: e.tensor_tensor(out=cT[:], in0=tv[:, 0:384].rearrange("p (c t) -> p c t", c=3),
                                                      in1=gat[:].unsqueeze(2).to_broadcast([128, 3, 128]), op=ALU.mult),
                     reads=[ps_tk[4], sm_tk], writes=[cT_tk])
                for kc in range(2):
                    S.op("pe", lambda e, kc=kc: e.matmul(ps[5][:, 0:384], lhsT=cT[:, kc, :], rhs=wq[:, kc, :],
                                                         start=(kc == 0), stop=(kc == 1)),
                         reads=[cT_tk, wi_tk], writes=[ps_tk[5]])
                S.op("pe", lambda e: e.matmul(ps[0][:, :], lhsT=cT[:, 2, :], rhs=wkv[:, :], start=True, stop=True),
                     reads=[cT_tk, wi_tk], writes=[ps_tk[0]])
                q3 = ps[5][:, 0:384].rearrange("p (h d) -> p h d", h=4)
                S.op("act", lambda e: e.activation(out=sq[:, 0:384], in_=ps[5][:, 0:384], func=AF.Square),
                     reads=[ps_tk[5]], writes=[sq_tk])
                S.op("dve", lambda e: e.tensor_reduce(out=st4[:, 0, :], in_=sq[:, 0:384].rearrange("p (h d) -> p h d", h=4),
                                                      axis=AX.X, op=ALU.add), reads=[sq_tk], writes=[st4_tk])
                emit_rstd(S, st4[:, 0, :], st4[:, 1, :], st4[:, 2, :], 96, st4_tk)
                S.op("dve", lambda e: e.tensor_tensor(out=qn[:], in0=q3,
                                                      in1=st4[:, 2, :].unsqueeze(2).to_broadcast([128, 4, 96]), op=ALU.mult),
                     reads=[ps_tk[5], st4_tk], writes=[qn_tk])
                S.op("pool", lambda e: e.tensor_tensor(out=qg[:], in0=qn[:],
                                                       in1=gq[:, 0:96].unsqueeze(1).to_broadcast([128, 4, 96]), op=ALU.mult),
                     reads=[qn_tk, sm_tk], writes=[qg_tk])
                rope_and_T(qg, tt, qT, lo, qT_tk)
                kv3 = ps[0][:, :].rearrange("p (h d) -> p h d", h=4)
                S.op("act", lambda e: e.activation(out=Va[:, tt, :, 0:64], in_=kv3[:, :, 64:128], func=AF.Copy),
                     reads=[ps_tk[0]], writes=[Va_tk[c]])
                S.op("act", lambda e: e.activation(out=sq[:, 0:256].rearrange("p (h d) -> p h d", h=4),
                                                   in_=kv3[:, :, 0:64], func=AF.Square),
                     reads=[ps_tk[0]], writes=[sq_tk])
                S.op("dve", lambda e: e.tensor_reduce(out=st4[:, 3, :], in_=sq[:, 0:256].rearrange("p (h d) -> p h d", h=4),
                                                      axis=AX.X, op=ALU.add), reads=[sq_tk], writes=[st4_tk])
                S.op("act", lambda e: e.activation(out=junk[:, 0:32], in_=krs[:], func=AF.Square, accum_out=stt[:, 6:7]),
                     reads=[krs_tk], writes=[junk_tk, stt_tk])
                S.op("dve", lambda e: e.tensor_scalar(st4[:, 3, :], st4[:, 3, :], stt[:, 6:7], None, ALU.add),
                     reads=[stt_tk, st4_tk], writes=[st4_tk])
                emit_rstd(S, st4[:, 3, :], st4[:, 4, :], st4[:, 5, :], 96, st4_tk)
                S.op("dve", lambda e: e.tensor_tensor(out=qn[:, :, 0:64], in0=kv3[:, :, 0:64],
                                                      in1=st4[:, 5, :].unsqueeze(2).to_broadcast([128, 4, 64]), op=ALU.mult),
                     reads=[ps_tk[0], st4_tk, qg_tk], writes=[qn_tk])
                S.op("dve", lambda e: e.tensor_tensor(out=qn[:, :, 64:96], in0=krs[:].unsqueeze(1).to_broadcast([128, 4, 32]),
                                                      in1=st4[:, 5, :].unsqueeze(2).to_broadcast([128, 4, 32]), op=ALU.mult),
                     reads=[krs_tk, st4_tk], writes=[qn_tk])
                S.op("pool", lambda e: e.tensor_tensor(out=qg[:], in0=qn[:],
                                                       in1=gq[:, 96:192].unsqueeze(1).to_broadcast([128, 4, 96]), op=ALU.mult),
                     reads=[qn_tk, sm_tk, qf_tk], writes=[qg_tk])
                rope_and_T(qg, tt, kT, tt * 128, kT_tk[c])

        A = attn_alloc(nc, S, st, P, ps_tk, lname, sbanks=(0, 1, 2))
        A4 = AttnCtx()
        A4.__dict__.update(A.__dict__)
        A4.Sb = [P[i] for i in (0, 1, 2, 6)]
        A4.Sb_tk = [ps_tk[i] for i in (0, 1, 2, 6)]
        tri = cb[:, C_TRI:C_TRI + 128]
        for c in range(NCH):
            prep_chunk(c)
            for h in range(4):
                softmax_unit(
                    nc, S, A, c,
                    kT_ap=lambda kt: kT[0:96, h, kt * 128:(kt + 1) * 128],
                    qT_ap=lambda a, b: qT[0:96, h, a:b],
                    kq_tk=lambda kt: (qT_tk, kT_tk[kt // 4]),
                    v_ap=lambda kt: Va[:, kt, h, :],
                    v_tk=lambda kt: Va_tk[kt // 4],
                    scale=float(96 ** -0.5), bias_fn=lambda kt: None, bias_tk=None,
                    tri=tri, ones_f=cf[:, F_ONE:F_ONE + 64], c_tk=cb_tk,
                    oT_dst_rows=oT_dst[h * 64:(h + 1) * 64, :])
            A4.ui, A4.si, A4.pi = A.ui, A.si, A.pi
            for h in range(4):
                sb_unit(nc, S, A4, X, c, skT, sqT, lambda kt: (sqT_tk, skT_tk[kt // 4]), sv,
                        lambda kt: sv_tk[kt // 4], h, cb, cb_tk, oT_dst[256 + h * 64:256 + (h + 1) * 64, :])
            A.ui, A.si, A.pi = A4.ui, A4.si, A4.pi
        S.barrier()


def build_attn0(seq):
    nc = bass.Bass("TRN2", target_bir_lowering=False)
    dt = lambda n, s, d, k="ExternalInput": nc.dram_tensor(n, s, d, kind=k).ap()
    x_src = dt("x_src", [seq, D], F32)
    pos_tm = dt("pos_tm", [128, seq // 128], I32)
    w0 = dt("w0", [D, W0_N], F32)
    wqu = dt("wqu", [256, 384], F32)
    wkvu = dt("wkvu", [128, 512], F32)
    gT0 = dt("gT0", [128, KC], F32)
    gA = dt("gA", [128, 3], F32)
    gqk0 = dt("gqk0", [128, 192], F32)
    cb16 = dt("cb16", [128, 640], BF16)
    cf32 = dt("cf32", [128, F_W], F32)
    oT = dt("oT", [512, seq], BF16, "ExternalOutput")
    with contextlib.ExitStack() as st:
        S = Sched(nc, st)
        P = psum_banks(nc, st)
        phase_attn0(nc, S, P, seq, x_src, pos_tm, w0, wqu, wkvu, gT0, gA, gqk0, cb16, cf32, oT, "a")
        S.finish()
    return nc


SEQ = 4096
BATCH = 4
_CACHE = {}


def _gT(v):
    v = np.asarray(v, np.float32)
    return np.ascontiguousarray(v.reshape(-1, 128).T)


def _rep(v):
    v = np.asarray(v, np.float32)
    return np.ascontiguousarray(np.broadcast_to(v[None, :], (128, v.shape[0])))


def _prog(name, fn, *a):
    key = (name,) + a
    if key not in _CACHE:
        _CACHE[key] = fn(*a)
    return _CACHE[key]


def _layer0_attn_inputs(inp, b, g, x_b, c16, f32):
    w_in = inp["l0_w_in"]
    w0 = np.concatenate([w_in[:, 0:416], w_in[:, 416 + 256 * g:416 + 256 * g + 256],
                         w_in[:, 928 + 256 * g:928 + 256 * g + 256],
                         w_in[:, 1440 + 256 * g:1440 + 256 * g + 256]], axis=1)
    qa = np.asarray(inp["l0_mla_q_a_norm"], np.float32)
    kva = np.asarray(inp["l0_mla_kv_a_norm"], np.float32)
    pos = np.asarray(inp["positions"][b], np.int32)
    return dict(
        x_src=x_b, pos_tm=np.ascontiguousarray(pos.reshape(-1, 128).T),
        w0=np.ascontiguousarray(w0, np.float32),
        wqu=np.ascontiguousarray(inp["l0_mla_w_q_up"][:, 384 * g:384 * g + 384], np.float32),
        wkvu=np.ascontiguousarray(inp["l0_mla_w_kv_up"][:, 512 * g:512 * g + 512], np.float32),
        gT0=_gT(inp["l0_attn_norm"]),
        gA=np.ascontiguousarray(np.stack([qa[0:128], qa[128:256], kva], 1)),
        gqk0=_rep(np.concatenate([inp["l0_mla_q_norm"], inp["l0_mla_k_norm"]])),
        cb16=c16, cf32=f32)


def _layer1_attn_inputs(inp, b, g, x_b, c16, f32):
    w_in = inp["l1_w_in"]
    w1 = np.concatenate([w_in[:, 512 * g:512 * g + 512], w_in[:, 1024 + 512 * g:1024 + 512 * g + 512],
                         w_in[:, 2048 + 512 * g:2048 + 512 * g + 512], w_in[:, 3072 + 8 * g:3072 + 8 * g + 8]], axis=1)
    return dict(
        x_src=x_b, w1=np.ascontiguousarray(w1, np.float32), gT1=_gT(inp["l1_attn_norm"]),
        fb=_rep(np.asarray(inp["l1_fox_f_bias"], np.float32)[8 * g:8 * g + 8]),
        gqk=_rep(np.concatenate([inp["l1_fox_q_norm"], inp["l1_fox_k_norm"]])),
        cb16=c16, cf32=f32)


_PERM0 = np.concatenate([np.arange(0, 256), np.arange(512, 768), np.arange(256, 512), np.arange(768, 1024)])


def _post_inputs(inp, layer, g, x_res, oT_all, c16):
    sel = np.zeros((128, 2), np.float32)
    sel[:, g] = 1.0
    w_o = np.asarray(inp[f"l{layer}_w_o"], np.float32)
    if layer == 0:
        w_o = w_o[_PERM0]
    return dict(x_res=np.ascontiguousarray(x_res), oT_all=oT_all, sel=sel, gTf=_gT(inp[f"l{layer}_ffn_norm"]),
                w_o=np.ascontiguousarray(w_o), w_g=np.ascontiguousarray(inp[f"l{layer}_w_gate"], np.float32),
                w_u=np.ascontiguousarray(inp[f"l{layer}_w_up"], np.float32),
                w_d=np.ascontiguousarray(inp[f"l{layer}_w_down"], np.float32), cb16=c16)


def kernel_unfused(**inp):
    inp = {k: np.asarray(v) for k, v in inp.items()}
    x = np.asarray(inp["x"], np.float32)
    c16, f32 = make_consts()
    H = SEQ // 2
    cores = list(range(NCORES))
    cur = x
    for layer in range(2):
        if layer == 0:
            nc = _prog("attn0", build_attn0, SEQ)
            maps = [_layer0_attn_inputs(inp, c // 2, c % 2, np.ascontiguousarray(cur[c // 2]), c16, f32) for c in cores]
        else:
            nc = _prog("attn1", build_attn1, SEQ)
            maps = [_layer1_attn_inputs(inp, c // 2, c % 2, np.ascontiguousarray(cur[c // 2]), c16, f32) for c in cores]
        res = run_bass_kernel_spmd(nc, maps, core_ids=cores)
        oT = [np.concatenate([res.results[2 * b]["oT"], res.results[2 * b + 1]["oT"]], axis=0) for b in range(BATCH)]
        nc = _prog("post", build_post, SEQ)
        maps = [_post_inputs(inp, layer, c % 2, cur[c // 2, (c % 2) * H:(c % 2 + 1) * H], oT[c // 2], c16) for c in cores]
        res = run_bass_kernel_spmd(nc, maps, core_ids=cores)
        cur = np.stack([np.concatenate([res.results[2 * b]["y"], res.results[2 * b + 1]["y"]], axis=0)
                        for b in range(BATCH)], axis=0)
    return cur.astype(np.float32)


def kernel(**inp):
    return kernel_unfused(**inp)
```
